# Optimizing a Trainium2 kernel written in Bass

```python
import jax, jax.numpy as jnp
from jax import lax
import numpy as np

D_MODEL = 2048
BATCH = 2
SEQ = 8192
DEPTH = 4

CHUNK = 64
N_MIXERS = 2
N_HEADS = 16
HEAD_DIM = D_MODEL // N_HEADS
Q_BLOCK = 128
SGU_CHUNK = 128
SGU_WIDTH = D_MODEL
SGU_GROUPS = 16
SGU_GROUP_DIM = SGU_WIDTH // SGU_GROUPS
D_FF = 4 * D_MODEL
N_FOX = (DEPTH + 1) // 2
N_SGU = DEPTH // 2
EPS = 1e-6
NEG_INF = -1e30

kernel_name = "fox_gmlp_interleaved_hybrid"


def rmsnorm(x, g):
    xf = x.astype(jnp.float32)
    y = xf * lax.rsqrt(jnp.mean(xf * xf, axis=-1, keepdims=True) + EPS)
    return (y * g.astype(jnp.float32)).astype(x.dtype)


def forgetting_attention(h, w_in, b_f, g_q, g_k, w_out):
    B, S, _ = h.shape
    proj = h @ w_in
    q, k, v, f_logit = jnp.split(proj, [D_MODEL, 2 * D_MODEL, 3 * D_MODEL], axis=-1)
    q = q.reshape(B, S, N_HEADS, HEAD_DIM)
    k = k.reshape(B, S, N_HEADS, HEAD_DIM)
    v = v.reshape(B, S, N_HEADS, HEAD_DIM)
    q = rmsnorm(q, g_q).astype(jnp.float32) * (HEAD_DIM ** -0.5)
    k = rmsnorm(k, g_k).astype(jnp.float32)
    log_f = jax.nn.log_sigmoid((f_logit + b_f).astype(jnp.float32))
    c = jnp.cumsum(log_f, axis=1).transpose(0, 2, 1)
    outs = []
    for i in range(S // Q_BLOCK):
        q0 = i * Q_BLOCK
        kend = q0 + Q_BLOCK
        qb = q[:, q0:kend]
        kb = k[:, :kend]
        vb = v[:, :kend]
        s = jnp.einsum('bqhd,bkhd->bhqk', qb, kb)
        s = s + (c[:, :, q0:kend, None] - c[:, :, None, :kend])
        q_pos = q0 + jnp.arange(Q_BLOCK)
        k_pos = jnp.arange(kend)
        mask = k_pos[None, :] <= q_pos[:, None]
        s = jnp.where(mask[None, None], s, NEG_INF)
        p = jax.nn.softmax(s, axis=-1)
        outs.append(jnp.einsum('bhqk,bkhd->bqhd', p.astype(vb.dtype), vb))
    o = jnp.concatenate(outs, axis=1).reshape(B, S, D_MODEL)
    return o @ w_out


def spatial_gating(h, w_in, b_in, g_v, w_s, b_s, w_out):
    B, S, _ = h.shape
    uv = jax.nn.gelu(h @ w_in + b_in)
    u, v = jnp.split(uv, 2, axis=-1)
    n_chunks = S // SGU_CHUNK
    v = v.reshape(B, n_chunks, SGU_CHUNK, SGU_GROUPS, SGU_GROUP_DIM)
    v = rmsnorm(v, g_v)
    tril = jnp.tril(jnp.ones((SGU_CHUNK, SGU_CHUNK), dtype=bool))
    w_causal = jnp.where(tril[None], w_s, jnp.zeros_like(w_s))
    mixed = jnp.einsum('gts,bnsgd->bntgd', w_causal, v)
    mixed = mixed + b_s.T[None, None, :, :, None]
    gated = u * mixed.reshape(B, S, SGU_WIDTH)
    return gated @ w_out


def squared_relu_mlp(h, w1, w2):
    a = jax.nn.relu(h @ w1)
    return (a * a) @ w2


def setup_inputs(seed: int = 0) -> dict:
    key = jax.random.key(seed)
    ks = jax.random.split(key, 20)
    f32 = jnp.float32
    d_in = D_MODEL ** -0.5
    x = jax.random.normal(ks[0], (BATCH, SEQ, D_MODEL), f32)
    g_mix = 1.0 + 0.05 * jax.random.normal(ks[1], (DEPTH, D_MODEL), f32)
    g_mlp = 1.0 + 0.05 * jax.random.normal(ks[2], (DEPTH, D_MODEL), f32)
    fox_w_in = d_in * jax.random.normal(ks[3], (N_FOX, D_MODEL, 3 * D_MODEL + N_HEADS), f32)
    fox_b_f = jnp.linspace(2.0, 6.0, N_HEADS, dtype=f32)[None, :] + 0.1 * jax.random.normal(ks[4], (N_FOX, N_HEADS), f32)
    fox_g_q = 1.0 + 0.05 * jax.random.normal(ks[5], (N_FOX, HEAD_DIM), f32)
    fox_g_k = 1.0 + 0.05 * jax.random.normal(ks[6], (N_FOX, HEAD_DIM), f32)
    fox_w_out = d_in * jax.random.normal(ks[7], (N_FOX, D_MODEL, D_MODEL), f32)
    sgu_w_in = d_in * jax.random.normal(ks[8], (N_SGU, D_MODEL, 2 * SGU_WIDTH), f32)
    sgu_b_in = 0.02 * jax.random.normal(ks[9], (N_SGU, 2 * SGU_WIDTH), f32)
    sgu_g_v = 1.0 + 0.05 * jax.random.normal(ks[10], (N_SGU, SGU_GROUPS, SGU_GROUP_DIM), f32)
    sgu_w_s = 0.5 * SGU_CHUNK ** -0.5 * jax.random.normal(ks[11], (N_SGU, SGU_GROUPS, SGU_CHUNK, SGU_CHUNK), f32)
    sgu_b_s = 1.0 + 0.1 * jax.random.normal(ks[12], (N_SGU, SGU_GROUPS, SGU_CHUNK), f32)
    sgu_w_out = SGU_WIDTH ** -0.5 * jax.random.normal(ks[13], (N_SGU, SGU_WIDTH, D_MODEL), f32)
    mlp_w1 = d_in * jax.random.normal(ks[14], (DEPTH, D_MODEL, D_FF), f32)
    mlp_w2 = 0.5 * D_FF ** -0.5 * jax.random.normal(ks[15], (DEPTH, D_FF, D_MODEL), f32)
    return {"x": x, "g_mix": g_mix, "g_mlp": g_mlp,
            "fox_w_in": fox_w_in, "fox_b_f": fox_b_f, "fox_g_q": fox_g_q, "fox_g_k": fox_g_k, "fox_w_out": fox_w_out,
            "sgu_w_in": sgu_w_in, "sgu_b_in": sgu_b_in, "sgu_g_v": sgu_g_v, "sgu_w_s": sgu_w_s, "sgu_b_s": sgu_b_s, "sgu_w_out": sgu_w_out,
            "mlp_w1": mlp_w1, "mlp_w2": mlp_w2}


def reference(x, g_mix, g_mlp, fox_w_in, fox_b_f, fox_g_q, fox_g_k, fox_w_out,
              sgu_w_in, sgu_b_in, sgu_g_v, sgu_w_s, sgu_b_s, sgu_w_out, mlp_w1, mlp_w2):
    for i in range(DEPTH):
        h = rmsnorm(x, g_mix[i])
        j = i // N_MIXERS
        if i % N_MIXERS == 0:
            x = x + forgetting_attention(h, fox_w_in[j], fox_b_f[j], fox_g_q[j], fox_g_k[j], fox_w_out[j])
        else:
            x = x + spatial_gating(h, sgu_w_in[j], sgu_b_in[j], sgu_g_v[j], sgu_w_s[j], sgu_b_s[j], sgu_w_out[j])
        h = rmsnorm(x, g_mlp[i])
        x = x + squared_relu_mlp(h, mlp_w1[i], mlp_w2[i])
    return x
```

```python
import numpy as np
import concourse.bass as bass
import concourse.mybir as mybir
from concourse.bass_utils import run_bass_kernel_spmd

F32 = mybir.dt.float32
BF16 = mybir.dt.bfloat16
AF = mybir.ActivationFunctionType
ALU = mybir.AluOpType
AX = mybir.AxisListType

D = 2048
NL = 4
H = 16
DH = 128
DFF = 8192
TC = 2048
TT = 1024
NTT = TC // TT
EPS = 1e-6
NCORES = 8
GRP = [[0, 1, 2, 3], [4, 5, 6, 7]]


def spans_of(j):
    return [j, 7 - j, 8 + j, 15 - j]


def span_home(s):
    if s < 4:
        return s, 0
    if s < 8:
        return 7 - s, 1
    if s < 12:
        return s - 8, 2
    return 15 - s, 3


class Prog:
    ENG = ["pe", "act", "dve", "pool", "sp"]

    def __init__(self, nc):
        self.nc = nc
        self.q = {e: [] for e in self.ENG}
        self.esem = {e: nc.alloc_semaphore(f"pg_{e}") for e in ["pe", "act", "dve", "pool"]}
        self.ecnt = {e: 0 for e in self.esem}
        self.seen = {e: {} for e in self.ENG}
        self.wtok = {}
        self.rtoks = {}
        self.dsem = {}
        self.dcnt = {}
        self.allmax = {}
        self.nops = 0

    def _deps(self, reads, writes):
        deps = []
        for k in reads:
            t = self.wtok.get(k)
            if t is not None:
                deps.append(t)
        for k in writes:
            t = self.wtok.get(k)
            if t is not None:
                deps.append(t)
            deps.extend(self.rtoks.get(k, {}).values())
        return deps

    def _commit(self, tok, reads, writes):
        sem, val, _ = tok
        for k in reads:
            d = self.rtoks.setdefault(k, {})
            o = d.get(sem.num)
            if o is None or o[1] < val:
                d[sem.num] = tok
        for k in writes:
            self.wtok[k] = tok
            self.rtoks[k] = {}
        o = self.allmax.get(sem.num)
        if o is None or o[1] < val:
            self.allmax[sem.num] = tok

    def _wait(self, e, tok):
        sem, val, src = tok
        if e == "pe" and src == "pe":
            return
        if self.seen[e].get(sem.num, 0) >= val:
            return
        self.seen[e][sem.num] = val
        self.q[e].append(lambda eng, sem=sem, val=val: eng.wait_ge(sem, val))

    def op(self, e, fn, reads=(), writes=(), extra=()):
        for t in self._deps(reads, writes) + list(extra):
            self._wait(e, t)
        self.ecnt[e] += 1
        val = self.ecnt[e]
        sem = self.esem[e]
        self.q[e].append(lambda eng, fn=fn, sem=sem: fn(eng).then_inc(sem, 1))
        tok = (sem, val, e)
        self._commit(tok, reads, writes)
        self.nops += 1
        return tok

    def mm(self, fns, reads=(), writes=(), extra=()):
        e = "pe"
        for t in self._deps(reads, writes) + list(extra):
            self._wait(e, t)
        for fn in fns[:-1]:
            self.q[e].append(fn)
        self.ecnt[e] += 1
        val = self.ecnt[e]
        sem = self.esem[e]
        last = fns[-1]
        self.q[e].append(lambda eng, fn=last, sem=sem: fn(eng).then_inc(sem, 1))
        tok = (sem, val, e)
        self._commit(tok, reads, writes)
        self.nops += len(fns)
        return tok

    def dma(self, e, fam, fns, reads=(), writes=(), extra=()):
        for t in self._deps(reads, writes) + list(extra):
            self._wait(e, t)
        if fam not in self.dsem:
            self.dsem[fam] = self.nc.alloc_semaphore(f"dm_{fam}")
            self.dcnt[fam] = 0
        sem = self.dsem[fam]
        for fn in fns:
            self.dcnt[fam] += 16
            self.q[e].append(lambda eng, fn=fn, sem=sem: fn(eng).then_inc(sem, 16))
        tok = (sem, self.dcnt[fam], "dma")
        self._commit(tok, reads, writes)
        self.nops += len(fns)
        return tok

    def coll(self, fam, fn, reads=(), writes=()):
        e = "pool"
        for t in self._deps(reads, writes):
            self._wait(e, t)
        if fam not in self.dsem:
            self.dsem[fam] = self.nc.alloc_semaphore(f"cc_{fam}")
            self.dcnt[fam] = 0
        sem = self.dsem[fam]
        self.dcnt[fam] += 1
        self.q[e].append(lambda eng, fn=fn, sem=sem: fn(eng).then_inc(sem, 1))
        tok = (sem, self.dcnt[fam], "dma")
        self._commit(tok, reads, writes)
        return tok

    def barrier(self, engines=None, skip_fams=()):
        skip = {self.dsem[f].num for f in skip_fams if f in self.dsem}
        toks = [t for n, t in self.allmax.items() if n not in skip]
        for e in (engines or self.ENG):
            for t in toks:
                self._wait(e, t)

    def finish(self):
        self.barrier()


class K:
    def __init__(self, plan, debug=False):
        self.plan = plan
        self.debug = debug
        nc = bass.Bass("TRN2", target_bir_lowering=False)
        self.nc = nc
        self.p = Prog(nc)
        self._declare()
        self._alloc()

    def _declare(self):
        nc = self.nc
        di = lambda n, s, dt=F32: nc.dram_tensor(n, s, dt, kind="ExternalInput").ap()
        self.x_in = di("x_t", [D, TC])
        self.y_out = nc.dram_tensor("y_t", [D, TC], F32, kind="ExternalOutput").ap()
        self.g_mix = di("g_mix", [128, NL * 16])
        self.g_mlp = di("g_mlp", [128, NL * 16])
        kinds = {k for k, _ in self.plan}
        dt_ = lambda n, s, dt: nc.dram_tensor(n, s, dt).ap()
        self.xres = dt_("xres", [D, TC], F32)
        if "mlp" in kinds:
            self.w1 = di("mlp_w1", [NL, D, DFF])
            self.w2 = di("mlp_w2", [NL, DFF, D])
        if "fox" in kinds:
            self.fox_wc = di("fox_wc", [2, D, 1540])
            self.fox_w_out = di("fox_w_out", [2, D, D])
            self.fox_bf = di("fox_bf", [4, 2])
            self.fox_gq = di("fox_gq", [128, 2])
            self.fox_gk = di("fox_gk", [128, 2])
            self.fox_gqr = di("fox_gqr", [1, 256])
            self.fox_gkr = di("fox_gkr", [1, 256])
            self.c_mask = di("c_mask", [128, 4 * 512])
            self.c_id4 = di("c_id4", [4, 4])
            self.c_selb = di("c_selb", [4, 4 * 128])
            self.hin = dt_("hin", [8, D, 256], BF16)
            self.hg = dt_("hg", [8, 4 * D, 256], BF16)
            dd = (lambda n, s, dt: nc.dram_tensor(n, s, dt, kind="ExternalOutput").ap()) if self.debug else dt_
            self.qd = dd("qd", [512, 8192], BF16)
            self.kd = dd("kd", [512, 8192], BF16)
            self.vd = dd("vd", [4 * 8192, DH], BF16)
            self.lfd = dd("lfd", [4, 8192], F32)
            if self.debug:
                self.dbg_cc = dd("dbg_cc", [4, 8192], F32)
                self.dbg_ckb = dd("dbg_ckb", [128, 256], F32)
                self.dbg_negb = dd("dbg_negb", [128, 2], F32)
                self.dbg_od = dd("dbg_od", [8, 512, 1024], BF16)
            self.od = dt_("od", [8, 512, 1024], BF16)
            self.og = dt_("og", [8, 4 * 512, 1024], BF16)
        if "sgu" in kinds:
            self.sgu_w_in = di("sgu_w_in", [2, D, 2 * D])
            self.sgu_w_out = di("sgu_w_out", [2, D, D])
            self.sgu_bu = di("sgu_bu", [128, 2 * 16])
            self.sgu_bv = di("sgu_bv", [2, 128, D])
            self.sgu_gv = di("sgu_gv", [2, 128, D])
            self.sgu_bs = di("sgu_bs", [2, 128, D])
            self.sgu_wst = di("sgu_wst", [2, 128, 16 * 128])
            self.c_tril = di("c_tril", [128, 128])

    def _alloc(self):
        nc = self.nc
        self.AR_BYTES = 180224
        self.arena = nc.alloc_sbuf_tensor("arena", [128, self.AR_BYTES // 4], F32)
        car = self.carve
        self.XT = car(0, [128, 16, TT], F32)
        self.HT = car(65536, [128, 16, TT], BF16)
        self.W1 = [car(98304 + i * 16384, [128, 16, 512], BF16) for i in range(2)]
        self.W2 = [car(131072 + i * 16384, [128, 4, 2048], BF16) for i in range(2)]
        self.A1 = [car(163840 + i * 8192, [128, 4, TT], BF16) for i in range(2)]
        sb = lambda n, s, dt: nc.alloc_sbuf_tensor(n, s, dt)
        self.RSTD = sb("rstd", [128, TT], F32)
        self.TMP = [sb(f"tmp{i}", [128, 512], F32) for i in range(4)]
        self.ONES = sb("ones", [128, 128], BF16)
        self.EPSC = sb("epsc", [128, 1], F32)
        self.ONEC = sb("onec", [128, 1], F32)
        kinds = {k for k, _ in self.plan}
        if "sgu" in kinds:
            self.WST = [sb(f"wst{i}", [128, 16, 128], BF16) for i in range(2)]
            self.BU = sb("bu", [128, 32], F32)
            self.VN = self.carve(147456, [128, 8, 512], BF16)
            self.BVQ = self.carve(147456 + 8192, [128, 512], F32)
            self.GVQ = self.carve(147456 + 10240, [128, 512], F32)
            self.BSQ = self.carve(147456 + 12288, [128, 4, 128], F32)
            self.SS4 = sb("ss4", [128, 4], F32)
            self.BVR = sb("bvr", [1, 512], BF16)
        if "fox" in kinds:
            self.ONEF = sb("onef", [128, 512], F32)
            self.GQ = sb("gq", [128, 2], F32)
            self.GK = sb("gk", [128, 2], F32)
            self.NBF = sb("nbf", [4, 2], F32)
            self.NEGB = sb("negb", [128, 2], F32)
            self.GROW = sb("grow", [1, 512], F32)
            self.NB1 = sb("nb1", [1, 2], F32)
            car = self.carve
            self.WQKV = [car(i * 16384, [128, 16, 512], BF16) for i in range(3)]
            self.HTB = [self.HT, car(98304, [128, 16, TT], BF16)]
            self.WF = car(49152, [128, 16, 4], BF16)
            self.KST = [car(131072 + i * 2048, [128, 1024], BF16) for i in range(4)]
            self.SQB = [car(139264 + i * 1024, [128, 512], BF16) for i in range(2)]
            self.RK = [car(141312 + i * 2048, [128, 512], F32) for i in range(2)]
            self.VST = [car(145408 + i * 4096, [128, 4, 4, 128], BF16) for i in range(2)]
            self.EL = car(153600, [128, 1024], F32)
            self.LFS = car(157696, [128, 1024], F32)
            self.KT = [car(i * 16384, [128, 8192], BF16) for i in range(2)]
            self.VH = [car(32768 + i * 16384, [128, 64, 128], BF16) for i in range(2)]
            self.QT = [car(65536 + i * 16384, [128, 8192], BF16) for i in range(2)]
            self.AH = [car(98304 + i * 2048, [128, 512], F32) for i in range(2)]
            self.TR = [car(102400 + i * 2048, [128, 512], F32) for i in range(3)]
            self.PT = [car(108544 + i * 1024, [128, 512], BF16) for i in range(3)]
            self.RL = [car(111616 + i * 2048, [128, 512], F32) for i in range(2)]
            self.OST = [car(115712 + i * 1024, [128, 512], BF16) for i in range(2)]
            self.CKB = car(117760, [128, 64, 4], F32)
            self.MASK = car(118784, [128, 4, 512], F32)
            self.CC = car(126976, [128, 8192], F32)
            self.SELB = car(159744, [128, 4, 128], F32)
            self.ID4 = car(161792, [128, 64], F32)
            self.WQ = [car(162048 + i * 2048, [128, 512], F32) for i in range(2)]
            self.T2 = [car(166144 + i * 2048, [128, 512], F32) for i in range(2)]
            self.BSP = [car(170240 + i * 256, [128, 64], F32) for i in range(2)]
            self.NAS = [car(170752 + i * 32, [128, 1], F32) for i in range(2)]
            self.LACC = [[car(171008 + (2 * i + k) * 2048, [128, 512], F32) for k in range(2)] for i in range(2)]
        self.GMIX = sb("gmix", [128, NL * 16], F32)
        self.GMLP = sb("gmlp", [128, NL * 16], F32)
        self.PS = [nc.alloc_psum_tensor(f"ps{i}", [128, 512], F32) for i in range(8)]

    def carve(self, off, shape, dt):
        esz = 4 if dt == F32 else 2
        n = int(np.prod(shape[1:]))
        assert off % 4 == 0 and (n * esz) % 4 == 0
        assert off + n * esz <= self.AR_BYTES, (off, shape)
        v = self.arena[:, off // 4: off // 4 + (n * esz) // 4]
        if dt != F32:
            v = v.bitcast(dt)
        if len(shape) == 3:
            v = v.rearrange("p (a b) -> p a b", a=shape[1])
        elif len(shape) == 4:
            v = v.rearrange("p (a b c) -> p a b c", a=shape[1], b=shape[2])
        return v

    def setup(self):
        p = self.p
        p.op("dve", lambda e: e.memset(self.ONES[:], 1.0), writes=["ONES"])
        p.dma("sp", "gmix", [lambda e: e.dma_start(out=self.GMIX[:], in_=self.g_mix[:, :])], writes=["GMIX"])
        p.dma("sp", "gmlp", [lambda e: e.dma_start(out=self.GMLP[:], in_=self.g_mlp[:, :])], writes=["GMLP"])
        p.op("dve", lambda e: e.memset(self.EPSC[:], EPS), writes=["EPSC"])

    def rsqrt_ps(self, out, ps, scale, pskey, outkey):
        p = self.p
        p.op("act", lambda e: e.activation(out=out, in_=ps, func=AF.Sqrt, bias=self.EPSC[0:out.shape[0], 0:1], scale=float(scale)),
             reads=[pskey, "EPSC"], writes=[outkey])
        p.op("dve", lambda e: e.reciprocal(out=out, in_=out), reads=[outkey], writes=[outkey])

    def xkeys(self):
        return [("XT", dc) for dc in range(16)]

    def load_xt(self, src, tt):
        v = src.rearrange("(dc p) t -> p dc t", p=128)
        for q4 in range(4):
            self.p.dma("sp", f"xl{q4}", [lambda e, q4=q4: e.dma_start(out=self.XT[:, q4 * 4:(q4 + 1) * 4, :],
                                                                     in_=v[:, q4 * 4:(q4 + 1) * 4, tt * TT:(tt + 1) * TT])],
                       reads=[(src.name, tt, q4)], writes=[("XT", dc) for dc in range(q4 * 4, q4 * 4 + 4)])

    def store_xt(self, dst, tt):
        v = dst.rearrange("(dc p) t -> p dc t", p=128)
        for q4 in range(4):
            self.p.dma("sp", f"xs{q4}", [lambda e, q4=q4: e.dma_start(out=v[:, q4 * 4:(q4 + 1) * 4, tt * TT:(tt + 1) * TT],
                                                                     in_=self.XT[:, q4 * 4:(q4 + 1) * 4, :])],
                       reads=[("XT", dc) for dc in range(q4 * 4, q4 * 4 + 4)], writes=[(dst.name, tt, q4)])

    def norm_to_ht(self, G, l):
        p = self.p
        HT, XT = self.HT, self.XT
        for q4 in range(4):
            p.op("act", lambda e, q4=q4: e.activation(out=HT[:, q4 * 4:(q4 + 1) * 4, :], in_=XT[:, q4 * 4:(q4 + 1) * 4, :], func=AF.Square),
                 reads=[("XT", dc) for dc in range(q4 * 4, q4 * 4 + 4)], writes=[("HT", dc) for dc in range(q4 * 4, q4 * 4 + 4)])
        for half in range(2):
            ps = self.PS[half]
            fns = [lambda e, dc=dc, ps=ps, half=half: e.matmul(ps[:], self.ONES[:], HT[:, dc, half * 512:(half + 1) * 512],
                                                               start=(dc == 0), stop=(dc == 15)) for dc in range(16)]
            p.mm(fns, reads=["ONES"] + [("HT", dc) for dc in range(16)], writes=[("PS", half)])
            self.rsqrt_ps(self.RSTD[:, half * 512:(half + 1) * 512], ps[:], 1.0 / D, ("PS", half), ("RSTD", half))
        for dc in range(16):
            p.op("dve", lambda e, dc=dc: e.scalar_tensor_tensor(out=HT[:, dc, :], in0=XT[:, dc, :], scalar=G[:, l * 16 + dc:l * 16 + dc + 1],
                                                                in1=self.RSTD[:], op0=ALU.mult, op1=ALU.mult),
                 reads=[("XT", dc), ("RSTD", 0), ("RSTD", 1), "GMIX", "GMLP"], writes=[("HT", dc)])

    def load_w1slab(self, slot, src2d, c0):
        v = src2d.rearrange("(dc p) f -> p dc f", p=128)
        self.p.dma("pool", f"w1_{slot}", [lambda e: e.dma_start(out=self.W1[slot][:], in_=v[:, :, c0:c0 + 512])],
                   writes=[("W1", slot)])

    def load_w2slab(self, slot, src2d, r0):
        v = src2d[r0:r0 + 512, :].rearrange("(fc p) d -> p fc d", p=128)
        self.p.dma("pool", f"w2_{slot}", [lambda e: e.dma_start(out=self.W2[slot][:], in_=v)],
                   writes=[("W2", slot)])

    def stage2(self, w2slot, act, act_keys, bank0=4):
        p = self.p
        W2 = self.W2[w2slot]
        n = 0
        for d2c in range(16):
            for half in range(2):
                b = bank0 + (n % 4)
                n += 1
                ps = self.PS[b]
                fns = [lambda e, fc=fc, ps=ps, d2c=d2c, half=half: e.matmul(ps[:], W2[:, fc, d2c * 128:(d2c + 1) * 128],
                                                                            act[:, fc, half * 512:(half + 1) * 512],
                                                                            start=(fc == 0), stop=(fc == 3)) for fc in range(4)]
                p.mm(fns, reads=[("W2", w2slot)] + act_keys, writes=[("PS", b)])
                p.op("dve", lambda e, ps=ps, d2c=d2c, half=half: e.tensor_tensor(out=self.XT[:, d2c, half * 512:(half + 1) * 512],
                                                                                 in0=self.XT[:, d2c, half * 512:(half + 1) * 512],
                                                                                 in1=ps[:], op=ALU.add),
                     reads=[("PS", b), ("XT", d2c)], writes=[("XT", d2c)])

    def mlp_s1(self, l, g):
        p = self.p
        slot = g % 2
        W1 = self.W1[slot]
        A1 = self.A1[slot]
        n = 0
        for fc in range(4):
            for half in range(2):
                b = n % 4
                tmp = self.TMP[n % 4]
                tk = ("TMP", n % 4)
                n += 1
                ps = self.PS[b]
                fns = [lambda e, dc=dc, ps=ps, fc=fc, half=half: e.matmul(ps[:], W1[:, dc, fc * 128:(fc + 1) * 128],
                                                                          self.HT[:, dc, half * 512:(half + 1) * 512],
                                                                          start=(dc == 0), stop=(dc == 15)) for dc in range(16)]
                p.mm(fns, reads=[("W1", slot)] + [("HT", dc) for dc in range(16)], writes=[("PS", b)])
                p.op("act", lambda e, ps=ps, tmp=tmp: e.activation(out=tmp[:], in_=ps[:], func=AF.Relu),
                     reads=[("PS", b)], writes=[tk])
                p.op("dve", lambda e, tmp=tmp, fc=fc, half=half: e.tensor_tensor(out=A1[:, fc, half * 512:(half + 1) * 512],
                                                                                 in0=tmp[:], in1=tmp[:], op=ALU.mult),
                     reads=[tk], writes=[("A1", slot)])

    def mlp(self, l):
        NG = DFF // 512
        w1l = self.w1[l]
        w2l = self.w2[l]
        self.norm_to_ht(self.GMLP, l)
        self.load_w1slab(0, w1l, 0)
        self.load_w2slab(0, w2l, 0)
        self.load_w1slab(1, w1l, 512)
        self.load_w2slab(1, w2l, 512)
        self.mlp_s1(l, 0)
        for g in range(NG):
            if g + 1 < NG:
                self.mlp_s1(l, g + 1)
            if g + 2 < NG:
                self.load_w1slab(g % 2, w1l, (g + 2) * 512)
            self.stage2(g % 2, self.A1[g % 2], [("A1", g % 2)])
            if g + 2 < NG:
                self.load_w2slab(g % 2, w2l, (g + 2) * 512)


    def sgu_setup(self):
        p = self.p
        p.dma("sp", "bu", [lambda e: e.dma_start(out=self.BU[:], in_=self.sgu_bu[:, :])], writes=["BU"])
        stg = self.carve(0, [128, 16, 128], F32)
        tril = self.carve(8192, [128, 128], F32)
        p.dma("sp", "xt", [lambda e: e.dma_start(out=tril, in_=self.c_tril[:, :])], writes=["TRIL"])
        for j in range(2):
            p.dma("sp", "xt", [lambda e, j=j: e.dma_start(out=stg, in_=self.sgu_wst[j].rearrange("p (g t) -> p g t", g=16))],
                  writes=["WSTG"])
            p.op("dve", lambda e, j=j: e.tensor_tensor(out=self.WST[j][:], in0=stg,
                                                      in1=tril.unsqueeze(1).to_broadcast([128, 16, 128]), op=ALU.mult),
                 reads=["WSTG", "TRIL"], writes=[("WST", j)])

    def sgu(self, j, l):
        p = self.p
        HT = self.HT
        win = self.sgu_w_in[j]
        wout = self.sgu_w_out[j]
        self.norm_to_ht(self.GMIX, l)
        HTK = [("HT", dc) for dc in range(16)]
        pend = None
        nb = 0
        for qd in range(4):
            self.load_w1slab(0, win, D + qd * 512)
            self.load_w1slab(1, win, qd * 512)
            p.dma("pool", "sgb", [lambda e, qd=qd: e.dma_start(out=self.BVR[0:1, :], in_=self.sgu_bv[j][0:1, qd * 512:(qd + 1) * 512])],
                  writes=["BVR"])
            p.dma("sp", "sgq", [
                lambda e, qd=qd: e.dma_start(out=self.GVQ, in_=self.sgu_gv[j][:, qd * 512:(qd + 1) * 512]),
                lambda e, qd=qd: e.dma_start(out=self.BSQ, in_=self.sgu_bs[j][:, qd * 512:(qd + 1) * 512].rearrange("p (g t) -> p g t", g=4)),
            ], writes=["SGQ"])
            Wv = self.W1[0]
            for tb in range(8):
                b = nb % 4
                nb += 1
                ps = self.PS[b]
                fns = [lambda e, dc=dc, ps=ps, tb=tb: e.matmul(ps[:], HT[:, dc, tb * 128:(tb + 1) * 128], Wv[:, dc, :],
                                                               start=(dc == 0), stop=False) for dc in range(16)]
                fns.append(lambda e, ps=ps: e.matmul(ps[:], self.ONES[0:1, :], self.BVR[0:1, :], start=False, stop=True))
                p.mm(fns, reads=[("W1", 0), "BVR", "ONES"] + HTK, writes=[("PS", b)])
                t1 = self.TMP[tb % 2]
                t2 = self.TMP[2 + tb % 2]
                t1k = ("TMP", tb % 2)
                t2k = ("TMP", 2 + tb % 2)
                p.op("act", lambda e, ps=ps, t1=t1: e.activation(out=t1[:], in_=ps[:], func=AF.Gelu_apprx_tanh),
                     reads=[("PS", b)], writes=[t1k])
                p.op("act", lambda e, t1=t1, t2=t2: e.activation(out=t2[:], in_=t1[:], func=AF.Square),
                     reads=[t1k], writes=[t2k])
                p.op("dve", lambda e, t2=t2: e.tensor_reduce(out=self.SS4[:], in_=t2[:].rearrange("p (g d) -> p g d", g=4), axis=AX.X, op=ALU.add),
                     reads=[t2k], writes=["SS4"])
                self.rsqrt_ps(self.SS4[:], self.SS4[:], 1.0 / 128, "SS4", "SS4")
                p.op("dve", lambda e, t1=t1: e.tensor_tensor(out=t1[:].rearrange("p (g d) -> p g d", g=4),
                                                             in0=t1[:].rearrange("p (g d) -> p g d", g=4),
                                                             in1=self.SS4[:].unsqueeze(2).to_broadcast([128, 4, 128]), op=ALU.mult),
                     reads=[t1k, "SS4"], writes=[t1k])
                p.op("dve", lambda e, tb=tb, t1=t1: e.tensor_tensor(out=self.VN[:, tb, :], in0=t1[:], in1=self.GVQ, op=ALU.mult),
                     reads=[t1k, "SGQ"], writes=[("VN", tb)])
            if pend is not None:
                self.stage2(0, self.A1[pend % 2], [("A1", pend % 2)])
                pend = None
            self.load_w2slab(0, wout, qd * 512)
            Wu = self.W1[1]
            GT = self.A1[qd % 2]
            for g in range(4):
                for half in range(2):
                    b = nb % 4
                    nb += 1
                    ps = self.PS[b]
                    fns = [lambda e, dc=dc, ps=ps, g=g, half=half: e.matmul(ps[:], Wu[:, dc, g * 128:(g + 1) * 128],
                                                                            HT[:, dc, half * 512:(half + 1) * 512],
                                                                            start=(dc == 0), stop=(dc == 15)) for dc in range(16)]
                    p.mm(fns, reads=[("W1", 1)] + HTK, writes=[("PS", b)])
                    tu = self.TMP[2]
                    p.op("act", lambda e, ps=ps, g=g, qd=qd: e.activation(out=tu[:], in_=ps[:], func=AF.Gelu_apprx_tanh,
                                                                          bias=self.BU[:, j * 16 + qd * 4 + g:j * 16 + qd * 4 + g + 1], scale=1.0),
                         reads=[("PS", b), "BU"], writes=[("TMP", 2)])
                    b2 = nb % 4
                    nb += 1
                    pm = self.PS[b2]
                    fns = [lambda e, pm=pm, g=g, half=half, tq=tq, qd=qd: e.matmul(
                        pm[:, tq * 128:(tq + 1) * 128], self.VN[:, half * 4 + tq, g * 128:(g + 1) * 128],
                        self.WST[j][:, qd * 4 + g, :], start=True, stop=True) for tq in range(4)]
                    p.mm(fns, reads=[("VN", half * 4 + tq) for tq in range(4)] + [("WST", j)], writes=[("PS", b2)])
                    tm = self.TMP[3]
                    p.op("dve", lambda e, pm=pm, g=g: e.tensor_tensor(out=tm[:].rearrange("p (c t) -> p c t", c=4),
                                                                      in0=pm[:].rearrange("p (c t) -> p c t", c=4),
                                                                      in1=self.BSQ[:, g, :].unsqueeze(1).to_broadcast([128, 4, 128]), op=ALU.add),
                         reads=[("PS", b2), "SGQ"], writes=[("TMP", 3)])
                    p.op("dve", lambda e, g=g, half=half, GT=GT: e.tensor_tensor(out=GT[:, g, half * 512:(half + 1) * 512], in0=tm[:], in1=tu[:], op=ALU.mult),
                         reads=[("TMP", 3), ("TMP", 2)], writes=[("A1", qd % 2)])
            pend = qd
        self.stage2(0, self.A1[pend % 2], [("A1", pend % 2)])

    def fox_setup(self):
        p = self.p
        p.op("dve", lambda e: e.memset(self.ONEF[:], 1.0), writes=["ONEF"])
        p.op("dve", lambda e: e.memset(self.ONEC[:], 1.0), writes=["ONEC"])
        p.dma("sp", "fxs", [
            lambda e: e.dma_start(out=self.GQ[:], in_=self.fox_gq[:, :]),
            lambda e: e.dma_start(out=self.GK[:], in_=self.fox_gk[:, :]),
            lambda e: e.dma_start(out=self.NBF[:], in_=self.fox_bf[:, :]),
            lambda e: e.dma_start(out=self.GROW[:, 0:256], in_=self.fox_gqr[:, :]),
            lambda e: e.dma_start(out=self.GROW[:, 256:512], in_=self.fox_gkr[:, :]),
        ], writes=["FXS"])
        p.op("dve", lambda e: e.tensor_scalar(out=self.GQ[:], in0=self.GQ[:], scalar1=float(DH ** -0.5), scalar2=None, op0=ALU.mult),
             reads=["FXS"], writes=["GQ"])
        p.op("dve", lambda e: e.tensor_scalar(out=self.NBF[:], in0=self.NBF[:], scalar1=-1.0, scalar2=None, op0=ALU.mult),
             reads=["FXS"], writes=["NBF"])
        p.op("dve", lambda e: e.tensor_tensor(out=self.GROW[:, 0:256], in0=self.GROW[:, 0:256], in1=self.GROW[:, 256:512], op=ALU.mult),
             reads=["FXS"], writes=["GROW"])
        p.op("dve", lambda e: e.tensor_reduce(out=self.NB1[:], in_=self.GROW[:, 0:256].rearrange("p (l d) -> p l d", l=2),
                                              axis=AX.X, op=ALU.max, apply_absolute_value=True),
             reads=["GROW"], writes=["NB1"])
        p.op("dve", lambda e: e.tensor_scalar(out=self.NB1[:], in0=self.NB1[:], scalar1=-float(np.sqrt(DH)), scalar2=None, op0=ALU.mult),
             reads=["NB1"], writes=["NB1"])
        ps = self.PS[0]
        p.mm([lambda e: e.matmul(ps[:, 0:2], self.ONEF[0:1, 0:128], self.NB1[0:1, 0:2], start=True, stop=True)],
             reads=["ONEF", "NB1"], writes=[("PS", 0)])
        p.op("dve", lambda e: e.tensor_copy(out=self.NEGB[:], in_=ps[:, 0:2]), reads=[("PS", 0)], writes=["NEGB"])

    def fox_norm_gather(self, j, l, src):
        p = self.p
        for tt in range(NTT):
            self.load_xt(src, tt)
            self.norm_to_ht(self.GMIX, l)
            fns = []
            for cc in range(4):
                c = tt * 4 + cc
                fns.append(lambda e, c=c, cc=cc: e.dma_start(out=self.hin[c].rearrange("(dc p) t -> p dc t", p=128),
                                                             in_=self.HT[:, :, cc * 256:(cc + 1) * 256]))
            p.dma("sp", "hst", fns, reads=[("HT", dc) for dc in range(16)], writes=[("hin", tt)])
            for cc in range(4):
                c = tt * 4 + cc
                p.coll("cc", lambda e, c=c: e.collective_compute("AllGather", ALU.bypass, replica_groups=GRP,
                                                                 ins=[self.hin[c]], outs=[self.hg[c]]),
                       reads=[("hin", tt)], writes=[("hg", c)])

    def fox_proj(self, j, l):
        p = self.p
        HT = self.HT
        HTK = [("HT", dc) for dc in range(16)]
        wc = self.fox_wc[j].rearrange("(dc p) f -> p dc f", p=128)
        for i in range(3):
            p.dma("pool", f"wqkv{i}", [lambda e, i=i: e.dma_start(out=self.WQKV[i], in_=wc[:, :, i * 512:(i + 1) * 512])],
                  writes=[("WQKV", i)])
        p.dma("pool", "wf", [lambda e: e.dma_start(out=self.WF, in_=wc[:, :, 1536:1540])], writes=["WF"])
        nb = 0
        nk = 0
        nr = 0
        nv = 0
        pending = []
        for gi, gt in enumerate([0, 2, 4, 6, 1, 3, 5, 7]):
            r, tt = divmod(gt, 2)
            hs = gi % 2
            HT = self.HTB[hs]
            HTK = [("HT" if hs == 0 else "HT2", dc) for dc in range(16)]
            fns = []
            for cc in range(4):
                c = tt * 4 + cc
                fns.append(lambda e, c=c, cc=cc, r=r, HT=HT: e.dma_start(out=HT[:, :, cc * 256:(cc + 1) * 256],
                                                                         in_=self.hg[c][r * D:(r + 1) * D, :].rearrange("(dc p) t -> p dc t", p=128)))
            p.dma("sp", f"htl{hs}", fns, reads=[("hg", tt * 4 + cc) for cc in range(4)], writes=HTK)
            for wi, G, dst in ((0, self.GQ, self.qd), (1, self.GK, self.kd)):
                W = self.WQKV[wi]
                for hh in range(4):
                    kst = self.KST[nk % 4]
                    kk = ("KST", nk % 4)
                    kfam = f"kst{nk % 4}"
                    nk += 1
                    for half in range(2):
                        b = nb % 4
                        nb += 1
                        ps = self.PS[b]
                        fns = [lambda e, dc=dc, ps=ps, hh=hh, half=half, W=W, HT=HT: e.matmul(ps[:], W[:, dc, hh * 128:(hh + 1) * 128],
                                                                                             HT[:, dc, half * 512:(half + 1) * 512],
                                                                                             start=(dc == 0), stop=(dc == 15)) for dc in range(16)]
                        p.mm(fns, reads=[("WQKV", wi)] + HTK, writes=[("PS", b)])
                        sq = self.SQB[nr % 2]
                        rk = self.RK[nr % 2]
                        sk = ("SQB", nr % 2)
                        rkk = ("RK", nr % 2)
                        nr += 1
                        p.op("act", lambda e, ps=ps, sq=sq: e.activation(out=sq, in_=ps[:], func=AF.Square), reads=[("PS", b)], writes=[sk])
                        while pending:
                            pending.pop(0)()

                        def post(b=b, ps=ps, sq=sq, rk=rk, sk=sk, rkk=rkk, kst=kst, kk=kk, kfam=kfam, half=half, G=G, hh=hh, gt=gt, dst=dst):
                            b2 = 4 + (b % 4)
                            ps2 = self.PS[b2]
                            p.mm([lambda e: e.matmul(ps2[:], self.ONES[:], sq, start=True, stop=True)],
                                 reads=[sk, "ONES"], writes=[("PS", b2)])
                            self.rsqrt_ps(rk, ps2[:], 1.0 / DH, ("PS", b2), rkk)
                            p.op("dve", lambda e: e.scalar_tensor_tensor(
                                out=kst[:, half * 512:(half + 1) * 512], in0=ps[:], scalar=G[:, j:j + 1], in1=rk, op0=ALU.mult, op1=ALU.mult),
                                 reads=[("PS", b), rkk, "GQ", "FXS"], writes=[kk])
                            if half == 1:
                                p.dma("sp", kfam, [lambda e: e.dma_start(out=dst[hh * 128:(hh + 1) * 128, gt * TT:(gt + 1) * TT], in_=kst)],
                                      reads=[kk], writes=[(dst.name, gt)])
                        pending.append(post)
            W = self.WQKV[2]
            for sp_ in range(2):
                vst = self.VST[nv % 2]
                vk = ("VST", nv % 2)
                nv += 1
                for i in range(4):
                    b = nb % 4
                    nb += 1
                    ps = self.PS[b]
                    c0 = sp_ * 512 + i
                    fns = [lambda e, dc=dc, ps=ps, c0=c0, HT=HT, W=W: e.matmul(ps[:], HT[:, dc, c0:c0 + 509:4], W[:, dc, :],
                                                                               start=(dc == 0), stop=(dc == 15)) for dc in range(16)]
                    p.mm(fns, reads=[("WQKV", 2)] + HTK, writes=[("PS", b)])
                    while pending:
                        pending.pop(0)()
                    p.op("act", lambda e, ps=ps, vst=vst, i=i: e.activation(out=vst[:, :, i, :], in_=ps[:].rearrange("p (h d) -> p h d", h=4), func=AF.Copy),
                         reads=[("PS", b)], writes=[vk])
                span = gt * 2 + sp_
                fns = []
                for hh in range(4):
                    r0 = hh * 8192 + span * 512
                    fns.append(lambda e, hh=hh, r0=r0, vst=vst: e.dma_start(
                        out=self.vd[r0:r0 + 512, :].rearrange("(p i) d -> p (i d)", i=4),
                        in_=vst[:, hh, :, :].rearrange("p i d -> p (i d)")))
                p.dma("sp", f"vst{(nv - 1) % 2}", fns, reads=[vk], writes=[("vd", span)])
            for half in range(2):
                b = nb % 4
                nb += 1
                ps = self.PS[b]
                fns = [lambda e, dc=dc, ps=ps, half=half, HT=HT: e.matmul(ps[0:4, :], self.WF[:, dc, :], HT[:, dc, half * 512:(half + 1) * 512],
                                                                          start=(dc == 0), stop=(dc == 15)) for dc in range(16)]
                p.mm(fns, reads=["WF"] + HTK, writes=[("PS", b)])
                el = self.EL[0:4, half * 512:(half + 1) * 512]
                p.op("act", lambda e, ps=ps, el=el: e.activation(out=el, in_=ps[0:4, :], func=AF.Exp, bias=self.NBF[:, j:j + 1], scale=-1.0),
                     reads=[("PS", b), "NBF"], writes=[("EL", half)])
                p.op("act", lambda e, el=el: e.activation(out=el, in_=el, func=AF.Ln, bias=self.ONEC[0:4, 0:1], scale=1.0),
                     reads=[("EL", half), "ONEC"], writes=[("EL", half)])
                p.op("dve", lambda e, el=el, half=half: e.tensor_scalar(out=self.LFS[0:4, half * 512:(half + 1) * 512], in0=el,
                                                                        scalar1=-1.0, scalar2=None, op0=ALU.mult),
                     reads=[("EL", half)], writes=["LFS"])
            p.dma("sp", "lfs", [lambda e, gt=gt: e.dma_start(out=self.lfd[:, gt * TT:(gt + 1) * TT], in_=self.LFS[0:4, :])],
                  reads=["LFS"], writes=[("lfd", gt)])

    def fox_attn(self, j, l):
        p = self.p
        p.dma("sp", "atc", [
            lambda e: e.dma_start(out=self.MASK, in_=self.c_mask.rearrange("p (i q) -> p i q", i=4)),
            lambda e: e.dma_start(out=self.SELB[0:4], in_=self.c_selb.rearrange("p (h m) -> p h m", h=4)),
            lambda e: e.dma_start(out=self.ID4[0:4, 0:4], in_=self.c_id4[:, :]),
            lambda e: e.dma_start(out=self.CC[0:4, :], in_=self.lfd[:, :]),
        ], reads=[("lfd", gt) for gt in range(8)], writes=["ATC", "CC"])
        for ch in range(16):
            init = 0.0 if ch == 0 else self.CC[0:4, ch * 512 - 1:ch * 512]
            p.op("dve", lambda e, ch=ch, init=init: e.tensor_tensor_scan(out=self.CC[0:4, ch * 512:(ch + 1) * 512], data0=self.ONEF[0:4, :],
                                                                         data1=self.CC[0:4, ch * 512:(ch + 1) * 512], initial=init,
                                                                         op0=ALU.mult, op1=ALU.add),
                 reads=["CC", "ONEF"], writes=["CC"])
        psck = self.PS[0]
        fns = []
        for kb in range(64):
            ks, i = divmod(kb, 4)
            fns.append(lambda e, kb=kb, ks=ks, i=i: e.matmul(psck[:, kb * 4:(kb + 1) * 4], self.CC[0:4, ks * 512 + i:ks * 512 + i + 509:4],
                                                             self.ID4[0:4, 0:4], start=True, stop=True))
        p.mm(fns, reads=["CC", "ATC"], writes=[("PS", 0)])
        p.op("dve", lambda e: e.tensor_scalar(out=self.CKB.rearrange("p k h -> p (k h)"), in0=psck[:, 0:256], scalar1=-1.0,
                                              scalar2=self.NEGB[:, j:j + 1], op0=ALU.mult, op1=ALU.add),
             reads=[("PS", 0), "NEGB"], writes=["CKB"])

        def load_head(hh):
            sl = hh % 2
            fk, fq = [], []
            for q4 in range(4):
                fk.append(lambda e, q4=q4: e.dma_start(out=self.KT[sl][:, q4 * 2048:(q4 + 1) * 2048], in_=self.kd[hh * 128:(hh + 1) * 128, q4 * 2048:(q4 + 1) * 2048]))
                fq.append(lambda e, q4=q4: e.dma_start(out=self.QT[sl][:, q4 * 2048:(q4 + 1) * 2048], in_=self.qd[hh * 128:(hh + 1) * 128, q4 * 2048:(q4 + 1) * 2048]))
            p.dma("sp", f"kt{sl}", fk, reads=[("kd", gt) for gt in range(8)], writes=[("KT", sl)])
            p.dma("sp", f"qt{sl}", fq, reads=[("qd", gt) for gt in range(8)], writes=[("QT", sl)])
            fv = []
            for q4 in range(4):
                fv.append(lambda e, q4=q4: e.dma_start(
                    out=self.VH[sl][:, q4 * 16:(q4 + 1) * 16, :].rearrange("p (s i) d -> p s (i d)", i=4),
                    in_=self.vd[hh * 8192 + q4 * 2048:hh * 8192 + (q4 + 1) * 2048, :].rearrange("(s p i) d -> p s (i d)", p=128, i=4)))
            p.dma("sp", f"vh{sl}", fv, reads=[("vd", sp) for sp in range(16)], writes=[("VH", sl)])

        load_head(0)
        st = {"nS": 0}

        def attn_span(hh, s, nsp):
            sl = hh % 2
            KT, VH, QT = self.KT[sl], self.VH[sl], self.QT[sl]
            ah = self.AH[nsp % 2]
            ahk = ("AH", nsp % 2)
            wq = self.WQ[nsp % 2]
            wqk = ("WQ", nsp % 2)
            bsp = self.BSP[nsp % 2]
            bspk = ("BSP", nsp % 2)
            nas = self.NAS[nsp % 2]
            nask = ("NAS", nsp % 2)
            po_f, pl_f, po_d, pl_d = self.PS[3], self.PS[4], self.PS[5], self.PS[6]
            kf, kd = [("PS", 3), ("PS", 4)], [("PS", 5), ("PS", 6)]
            rl, ost = self.RL[nsp % 2], self.OST[nsp % 2]
            rlk, ostk = ("RL", nsp % 2), ("OST", nsp % 2)
            t2 = self.T2[nsp % 2]
            t2k = ("T2", nsp % 2)
            pa = self.PS[7]
            p.mm([lambda e: e.matmul(pa[:], self.SELB[0:4, hh, :], self.CC[0:4, s * 512:(s + 1) * 512], start=True, stop=True)],
                 reads=["CC", "ATC"], writes=[("PS", 7)])
            p.op("act", lambda e: e.activation(out=ah, in_=pa[:], func=AF.Copy), reads=[("PS", 7)], writes=[ahk])
            noff = 4 * s
            if noff:
                p.op("dve", lambda e: e.tensor_scalar(out=nas, in0=ah[:, 0:1], scalar1=-1.0, scalar2=None, op0=ALU.mult),
                     reads=[ahk], writes=[nask])
                p.op("act", lambda e: e.activation(out=wq, in_=ah, func=AF.Exp, bias=nas, scale=1.0), reads=[ahk, nask], writes=[wqk])
                p.op("dve", lambda e: e.tensor_scalar(out=bsp[:, 0:noff], in0=self.CKB[:, 0:noff, hh], scalar1=ah[:, 0:1], scalar2=None, op0=ALU.add),
                     reads=[ahk, "CKB"], writes=[bspk])
            q_ap = QT[:, s * 512:(s + 1) * 512]

            def emit_S(kb):
                r = st["nS"] % 3
                st["nS"] += 1
                ks, i = divmod(kb, 4)
                ps = self.PS[r]
                p.mm([lambda e: e.matmul(ps[:], KT[:, ks * 512 + i:ks * 512 + i + 509:4], q_ap, start=True, stop=True)],
                     reads=[("KT", sl), ("QT", sl)], writes=[("PS", r)])
                pt = self.PT[r]
                if ks == s:
                    tr = self.TR[r]
                    p.op("dve", lambda e: e.tensor_tensor(out=tr, in0=ps[:], in1=ah, op=ALU.add),
                         reads=[("PS", r), ahk], writes=[("TR", r)])
                    p.op("dve", lambda e: e.tensor_tensor(out=tr, in0=tr, in1=self.MASK[:, i, :], op=ALU.add),
                         reads=[("TR", r), "ATC"], writes=[("TR", r)])
                    p.op("act", lambda e: e.activation(out=pt, in_=tr, func=AF.Exp, bias=self.CKB[:, kb, hh:hh + 1], scale=1.0),
                         reads=[("TR", r), "CKB"], writes=[("PT", r)])
                else:
                    p.op("act", lambda e: e.activation(out=pt, in_=ps[:], func=AF.Exp, bias=bsp[:, kb:kb + 1], scale=1.0),
                         reads=[("PS", r), bspk], writes=[("PT", r)])
                return r

            lacc = self.LACC[nsp % 2]

            def emit_PV(kb, r, first, last):
                pt = self.PT[r]
                diag = (kb // 4 == s)
                if diag:
                    p.mm([lambda e: e.matmul(po_d[:], VH[:, kb, :], pt, start=first, stop=last),
                          lambda e: e.matmul(pl_d[:], self.ONES[:], pt, start=first, stop=last)],
                         reads=[("PT", r), ("VH", sl), "ONES"], writes=kd)
                else:
                    if kb % 3 == 2:
                        p.mm([lambda e: e.matmul(po_f[:], VH[:, kb, :], pt, start=first, stop=last),
                              lambda e: e.matmul(pl_f[:], self.ONES[:], pt, start=(kb == 2), stop=False)],
                             reads=[("PT", r), ("VH", sl), "ONES"], writes=kf)
                    else:
                        par = (kb % 3)
                        la = lacc[par]
                        lak = ("LACC", nsp % 2, par)
                        if kb < 2:
                            p.op("dve", lambda e: e.tensor_copy(out=la, in_=pt), reads=[("PT", r)], writes=[lak])
                        else:
                            p.op("dve", lambda e: e.tensor_tensor(out=la, in0=la, in1=pt, op=ALU.add), reads=[("PT", r), lak], writes=[lak])
                        p.mm([lambda e: e.matmul(po_f[:], VH[:, kb, :], pt, start=first, stop=last)],
                             reads=[("PT", r), ("VH", sl)], writes=[kf[0]])

            order = list(range(noff, noff + 4)) + list(range(noff))
            n = len(order)
            ring = {}
            for t in range(min(2, n)):
                ring[t] = emit_S(order[t])
            for t in range(n):
                if t + 2 < n:
                    ring[t + 2] = emit_S(order[t + 2])
                kb = order[t]
                if kb >= noff:
                    first, last = (kb == noff), (kb == noff + 3)
                else:
                    first, last = (kb == 0), (kb == noff - 1)
                emit_PV(kb, ring.pop(t), first, last)
            if noff:
                p.mm([lambda e: e.matmul(pl_f[:], self.ONEF[:, 0:128], lacc[0], start=False, stop=False),
                      lambda e: e.matmul(pl_f[:], self.ONEF[:, 0:128], lacc[1], start=False, stop=True)],
                     reads=["ONEF", ("LACC", nsp % 2, 0), ("LACC", nsp % 2, 1)], writes=[kf[1]])
                p.op("dve", lambda e: e.tensor_tensor(out=rl, in0=pl_f[:], in1=wq, op=ALU.mult), reads=[kf[1], wqk], writes=[rlk])
                p.op("dve", lambda e: e.tensor_tensor(out=rl, in0=rl, in1=pl_d[:], op=ALU.add), reads=[rlk, kd[1]], writes=[rlk])
                p.op("dve", lambda e: e.reciprocal(out=rl, in_=rl), reads=[rlk], writes=[rlk])
                p.op("dve", lambda e: e.tensor_tensor(out=t2, in0=po_f[:], in1=wq, op=ALU.mult), reads=[kf[0], wqk], writes=[t2k])
                p.op("dve", lambda e: e.tensor_tensor(out=t2, in0=t2, in1=po_d[:], op=ALU.add), reads=[t2k, kd[0]], writes=[t2k])
                p.op("dve", lambda e: e.tensor_tensor(out=ost, in0=t2, in1=rl, op=ALU.mult), reads=[t2k, rlk], writes=[ostk])
            else:
                p.op("dve", lambda e: e.reciprocal(out=rl, in_=pl_d[:]), reads=[kd[1]], writes=[rlk])
                p.op("dve", lambda e: e.tensor_tensor(out=ost, in0=po_d[:], in1=rl, op=ALU.mult), reads=[kd[0], rlk], writes=[ostk])
            c, hf = divmod(s, 2)
            p.dma("sp", f"ost{nsp % 2}", [lambda e: e.dma_start(
                out=self.od[c][hh * 128:(hh + 1) * 128, hf * 512:(hf + 1) * 512], in_=ost)],
                  reads=[ostk], writes=[("od", c)])

        nsp = 0
        for hh in range(4):
            if hh + 1 < 4:
                load_head(hh + 1)
            for s in range(16):
                attn_span(hh, s, nsp)
                nsp += 1
        if self.debug:
            p.dma("sp", "dbg", [
                lambda e: e.dma_start(out=self.dbg_cc[:, :], in_=self.CC[0:4, :]),
                lambda e: e.dma_start(out=self.dbg_ckb[:, :], in_=self.CKB.rearrange("p k h -> p (k h)")),
                lambda e: e.dma_start(out=self.dbg_negb[:, :], in_=self.NEGB[:]),
            ] + [lambda e, c=c: e.dma_start(out=self.dbg_od[c], in_=self.od[c]) for c in range(8)],
                  reads=["CC", "CKB", "NEGB"] + [("od", c) for c in range(8)], writes=["DBG"])
        for c in range(8):
            p.coll("cc", lambda e, c=c: e.collective_compute("AllGather", ALU.bypass, replica_groups=GRP,
                                                             ins=[self.od[c]], outs=[self.og[c]]),
                   reads=[("od", c)], writes=[("og", c)])

    def fox_out(self, j, tt):
        p = self.p
        HT = self.HT

        def ld(e):
            pid = e.partition_id()
            idx = (pid % 4) * 2 + tt
            return e.dma_start(out=HT, in_=self.og[bass.ds(idx, 1), :, :].rearrange("o (hc p) t -> p (o hc) t", p=128))
        p.dma("sp", "htl", [ld], reads=[("og", c) for c in range(8)], writes=[("HT", dc) for dc in range(16)])
        wout = self.fox_w_out[j]
        self.load_w2slab(0, wout, 0)
        self.load_w2slab(1, wout, 512)
        for qd in range(4):
            self.stage2(qd % 2, HT[:, qd * 4:(qd + 1) * 4, :], [("HT", dc) for dc in range(qd * 4, qd * 4 + 4)])
            if qd + 2 < 4:
                self.load_w2slab(qd % 2, wout, (qd + 2) * 512)

    def build(self):
        p = self.p
        self.setup()
        kinds = {k for k, _ in self.plan}
        if "sgu" in kinds:
            self.sgu_setup()
        if "fox" in kinds:
            self.fox_setup()
        p.barrier()
        plan = self.plan
        i = 0
        nsub = len(plan)
        src = self.x_in
        while i < nsub:
            kind, l = plan[i]
            group = [plan[i]]
            if kind != "mlp" and i + 1 < nsub and plan[i + 1][0] == "mlp":
                group.append(plan[i + 1])
            i += len(group)
            dst = self.y_out if i >= nsub else self.xres
            if group[0][0] == "fox":
                j = group[0][1] // 2
                self.fox_norm_gather(j, group[0][1], src)
                p.barrier(skip_fams=("cc",))
                self.fox_proj(j, group[0][1])
                p.barrier()
                self.fox_attn(j, group[0][1])
                p.barrier()
            for tt in range(NTT):
                self.load_xt(src, tt)
                for kind, l in group:
                    if kind == "mlp":
                        self.mlp(l)
                    elif kind == "sgu":
                        self.sgu(l // 2, l)
                    elif kind == "fox":
                        self.fox_out(l // 2, tt)
                self.store_xt(dst, tt)
            p.barrier()
            src = dst
        p.finish()
        return self.emit()

    def emit(self):
        nc = self.nc
        p = self.p
        with nc.Block() as block:
            @block.tensor
            def _(eng):
                for f in p.q["pe"]:
                    f(eng)

            @block.scalar
            def _(eng):
                for f in p.q["act"]:
                    f(eng)

            @block.vector
            def _(eng):
                for f in p.q["dve"]:
                    f(eng)

            @block.gpsimd
            def _(eng):
                for f in p.q["pool"]:
                    f(eng)

            @block.sync
            def _(eng):
                for f in p.q["sp"]:
                    f(eng)
        return nc


def _fm(v):
    v = np.asarray(v, np.float32).reshape(-1, 16, 128)
    return np.ascontiguousarray(v.transpose(2, 0, 1).reshape(128, -1))


def core_tokens(c):
    b, j = divmod(c, 4)
    return b, np.arange(j * TC, (j + 1) * TC)


def host_inputs(inputs, plan):
    kinds = {k for k, _ in plan}
    f32 = lambda a: np.ascontiguousarray(np.asarray(a, np.float32))
    x = f32(inputs["x"])
    common = {"g_mix": _fm(inputs["g_mix"]), "g_mlp": _fm(inputs["g_mlp"])}
    if "mlp" in kinds:
        common["mlp_w1"] = f32(inputs["mlp_w1"])
        common["mlp_w2"] = f32(inputs["mlp_w2"])
    if "sgu" in kinds:
        rep = lambda v: np.ascontiguousarray(np.broadcast_to(f32(v).reshape(2, 1, D), (2, 128, D)))
        b_in = f32(inputs["sgu_b_in"])
        common["sgu_w_in"] = f32(inputs["sgu_w_in"])
        common["sgu_w_out"] = f32(inputs["sgu_w_out"])
        common["sgu_bu"] = _fm(b_in[:, :D])
        common["sgu_bv"] = rep(b_in[:, D:])
        common["sgu_gv"] = rep(f32(inputs["sgu_g_v"]).reshape(2, D))
        common["sgu_bs"] = rep(f32(inputs["sgu_b_s"]).reshape(2, D))
        ws = f32(inputs["sgu_w_s"])
        common["sgu_wst"] = np.ascontiguousarray(ws.transpose(0, 3, 1, 2).reshape(2, 128, 16 * 128))
        common["c_tril"] = np.triu(np.ones((128, 128), np.float32))
    if "fox" in kinds:
        common["fox_w_out"] = f32(inputs["fox_w_out"])
        common["fox_gq"] = np.ascontiguousarray(f32(inputs["fox_g_q"]).T)
        common["fox_gk"] = np.ascontiguousarray(f32(inputs["fox_g_k"]).T)
        common["fox_gqr"] = f32(inputs["fox_g_q"]).reshape(1, 256)
        common["fox_gkr"] = f32(inputs["fox_g_k"]).reshape(1, 256)
        pp = np.arange(128)[:, None, None]
        ii = np.arange(4)[None, :, None]
        qq = np.arange(512)[None, None, :]
        common["c_mask"] = np.where(4 * pp + ii <= qq, 0.0, -1e9).astype(np.float32).reshape(128, 2048)
        common["c_id4"] = np.eye(4, dtype=np.float32)
        selb = np.zeros((4, 4, 128), np.float32)
        for h in range(4):
            selb[h, h, :] = 1.0
        common["c_selb"] = selb.reshape(4, 512)
    maps = []
    for c in range(NCORES):
        b, idx = core_tokens(c)
        j = c % 4
        m = dict(common)
        m["x_t"] = np.ascontiguousarray(x[b, idx, :].T)
        if "fox" in kinds:
            w = f32(inputs["fox_w_in"])
            cs = slice(j * 512, (j + 1) * 512)
            m["fox_wc"] = np.ascontiguousarray(np.concatenate(
                [w[:, :, 0:D][:, :, cs], w[:, :, D:2 * D][:, :, cs], w[:, :, 2 * D:3 * D][:, :, cs],
                 w[:, :, 3 * D + 4 * j:3 * D + 4 * j + 4]], axis=2))
            m["fox_bf"] = np.ascontiguousarray(f32(inputs["fox_b_f"])[:, 4 * j:4 * j + 4].T)
        maps.append(m)
    return maps


def assemble(results):
    out = np.empty((2, 8192, D), np.float32)
    for c in range(NCORES):
        b, idx = core_tokens(c)
        out[b, idx, :] = results[c]["y_t"].T
    return out


FULL_PLAN = [("fox", 0), ("mlp", 0), ("sgu", 1), ("mlp", 1), ("fox", 2), ("mlp", 2), ("sgu", 3), ("mlp", 3)]


def kernel(**inputs):
    plan = FULL_PLAN
    k = K(plan)
    nc = k.build()
    maps = host_inputs(inputs, plan)
    res = run_bass_kernel_spmd(nc, maps, core_ids=list(range(NCORES)))
    return assemble(res.results)
```
